# Optimizing a Trainium2 kernel written in Bass

```python
import math
import jax, jax.numpy as jnp
from jax import lax
import numpy as np

D_MODEL = 2048
BATCH = 4
SEQ = 4096
DEPTH = 4

GRID_W = 64
CTX_LEN = 256
NORM_EPS = 1e-6
N_EVEN = (DEPTH + 1) // 2
N_ODD = DEPTH // 2
MIX_WIDTH = D_MODEL
SSD_INNER = MIX_WIDTH // 2
SSD_HEAD_DIM = 64
SSD_HEADS = SSD_INNER // SSD_HEAD_DIM
SSD_GROUPS = 2
SSD_HPG = SSD_HEADS // SSD_GROUPS
SSD_STATE = 128
SSD_CHUNK = 128
CONV_K = 5
XBC_DIM = SSD_INNER + 2 * SSD_GROUPS * SSD_STATE
SGU_WIDTH = MIX_WIDTH - SSD_INNER
SGU_CHUNK = 128
SGU_GROUP_DIM = 128
SGU_GROUPS = SGU_WIDTH // SGU_GROUP_DIM
Z_END = SSD_INNER
XBC_END = Z_END + XBC_DIM
DT_END = XBC_END + 2 * SSD_HEADS
U_END = DT_END + SGU_WIDTH
EVEN_IN_COLS = U_END + SGU_WIDTH
ATT_HEAD_DIM = 128
ATT_HEADS = MIX_WIDTH // ATT_HEAD_DIM
ATT_KV_HEADS = ATT_HEADS // 4
ATT_GROUP = ATT_HEADS // ATT_KV_HEADS
Q_COLS = ATT_HEADS * ATT_HEAD_DIM
KV_COLS = ATT_KV_HEADS * ATT_HEAD_DIM
QKV_COLS = Q_COLS + 2 * KV_COLS
WINDOW = 128
ATT_BLOCK = 128
ROPE_BASE = 10000.0
NEG_INF = -1e30
FFN_HIDDEN = -(-(8 * D_MODEL) // (3 * 256)) * 256

kernel_name = 'hybrid_ssd_sgu_swa_diffusion_trunk'


def rms_norm(x, w):
    xf = x.astype(jnp.float32)
    y = xf * lax.rsqrt(jnp.mean(xf * xf, axis=-1, keepdims=True) + NORM_EPS)
    return (y * w.astype(jnp.float32)).astype(x.dtype)


def swiglu(h, w_in, w_out):
    g, u = jnp.split(h @ w_in, 2, axis=-1)
    return (jax.nn.silu(g) * u) @ w_out


def dwconv_centred(x, w, b):
    y = lax.conv_general_dilated(x, w.astype(x.dtype)[:, None, :], window_strides=(1,),
                                 padding=[(CONV_K // 2, CONV_K // 2)],
                                 dimension_numbers=('NWC', 'WIO', 'NWC'),
                                 feature_group_count=x.shape[-1])
    return y + b.astype(x.dtype)


def axial_rope_tables(S):
    rows = S // GRID_W
    row = jnp.repeat(jnp.arange(rows, dtype=jnp.float32), GRID_W)
    col = jnp.tile(jnp.arange(GRID_W, dtype=jnp.float32), rows)
    quarter = ATT_HEAD_DIM // 4
    inv = ROPE_BASE ** (-jnp.arange(quarter, dtype=jnp.float32) / quarter)
    ang = jnp.stack([row[:, None] * inv, col[:, None] * inv], axis=1)
    return jnp.cos(ang), jnp.sin(ang)


def axial_rope(x, cos, sin):
    b, S, H, d = x.shape
    xs = x.reshape(b, S, H, 2, 2, d // 4)
    x1, x2 = xs[..., 0, :], xs[..., 1, :]
    cs = cos[:, None].astype(x.dtype)
    sn = sin[:, None].astype(x.dtype)
    return jnp.stack([x1 * cs - x2 * sn, x2 * cs + x1 * sn], axis=-2).reshape(b, S, H, d)


def ssd_chunked(xh, dt, a, bm, cm, init_state, need_y):
    b, L, G, J, P = xh.shape
    N = bm.shape[-1]
    Q = SSD_CHUNK
    nc = L // Q
    X = (xh * dt[..., None]).reshape(b, nc, Q, G, J, P)
    Bc = bm.reshape(b, nc, Q, G, N)
    Cc = cm.reshape(b, nc, Q, G, N)
    a_cs = jnp.cumsum((dt * a).reshape(b, nc, Q, G, J), axis=2)
    a_last = a_cs[:, :, -1]
    states = jnp.einsum('bcsgn,bcsgjp->bcgjpn', Bc, X * jnp.exp(a_last[:, :, None] - a_cs)[..., None])

    def step(s, inp):
        decay, st = inp
        return s * decay[..., None, None] + st, s

    final, s_prev = lax.scan(step, init_state, (jnp.moveaxis(jnp.exp(a_last), 1, 0), jnp.moveaxis(states, 1, 0)))
    if not need_y:
        return None, final
    s_prev = jnp.moveaxis(s_prev, 0, 1)
    seg = a_cs[:, :, :, None] - a_cs[:, :, None]
    lower = jnp.tril(jnp.ones((Q, Q), bool))[None, None, :, :, None, None]
    decay_ls = jnp.where(lower, jnp.exp(jnp.where(lower, seg, 0.0)), 0.0)
    cb = jnp.einsum('bclgn,bcsgn->bclsg', Cc, Bc)
    y_diag = jnp.einsum('bclsgj,bcsgjp->bclgjp', cb[..., None] * decay_ls, X)
    y_off = jnp.einsum('bclgn,bcgjpn->bclgjp', Cc, s_prev) * jnp.exp(a_cs)[..., None]
    return (y_diag + y_off).reshape(b, L, G, J, P), final


def ssd_inputs(p, conv_w, conv_b, dt_bias):
    b, L, _ = p.shape
    xbc = jax.nn.silu(dwconv_centred(p[..., :XBC_DIM], conv_w, conv_b)).astype(jnp.float32)
    gn = SSD_GROUPS * SSD_STATE
    xs = xbc[..., :SSD_INNER].reshape(b, L, SSD_GROUPS, SSD_HPG, SSD_HEAD_DIM)
    bm = xbc[..., SSD_INNER:SSD_INNER + gn].reshape(b, L, SSD_GROUPS, SSD_STATE)
    cm = xbc[..., SSD_INNER + gn:].reshape(b, L, SSD_GROUPS, SSD_STATE)
    dt = jax.nn.softplus(p[..., XBC_DIM:].astype(jnp.float32).reshape(b, L, 2, SSD_GROUPS, SSD_HPG)
                         + dt_bias.astype(jnp.float32).reshape(2, SSD_GROUPS, SSD_HPG))
    return xs, bm, cm, dt


def ssd_bidirectional(ctx_in, lat_in, a_log, need_ctx):
    xc, bc, cc, dtc = ctx_in
    xl, bl, cl, dtl = lat_in
    b = xl.shape[0]
    y_ctx, y_lat = None, None
    for d in range(2):
        a = -jnp.exp(a_log[d].astype(jnp.float32)).reshape(SSD_GROUPS, SSD_HPG)
        fl = (lambda t: jnp.flip(t, axis=1)) if d == 1 else (lambda t: t)
        init = jnp.zeros((b, SSD_GROUPS, SSD_HPG, SSD_HEAD_DIM, SSD_STATE), jnp.float32)
        yc, s_ctx = ssd_chunked(fl(xc), fl(dtc[:, :, d]), a, fl(bc), fl(cc), init, need_ctx)
        yl, _ = ssd_chunked(fl(xl), fl(dtl[:, :, d]), a, fl(bl), fl(cl), s_ctx, True)
        y_lat = fl(yl) if y_lat is None else y_lat + fl(yl)
        if need_ctx:
            y_ctx = fl(yc) if y_ctx is None else y_ctx + fl(yc)
    return y_ctx, y_lat


def ssd_finish(y, xs, z, d_skip, norm_w):
    b, L = y.shape[:2]
    y = y + xs * d_skip.astype(jnp.float32).reshape(SSD_GROUPS, SSD_HPG)[..., None]
    y = y.reshape(b, L, SSD_INNER) * jax.nn.silu(z.astype(jnp.float32))
    yg = y.reshape(b, L, SSD_GROUPS, SSD_INNER // SSD_GROUPS)
    yg = yg * lax.rsqrt(jnp.mean(yg * yg, axis=-1, keepdims=True) + NORM_EPS)
    return (yg.reshape(b, L, SSD_INNER) * norm_w.astype(jnp.float32)).astype(z.dtype)


def spatial_gating(u, v, sgu_w, sgu_b):
    b, L, _ = u.shape
    u = jax.nn.gelu(u)
    vf = jax.nn.gelu(v).astype(jnp.float32)
    mu = jnp.mean(vf, axis=-1, keepdims=True)
    var = jnp.mean(jnp.square(vf - mu), axis=-1, keepdims=True)
    vn = ((vf - mu) * lax.rsqrt(var + NORM_EPS)).astype(v.dtype)
    vc = vn.reshape(b, L // SGU_CHUNK, SGU_CHUNK, SGU_GROUPS, SGU_GROUP_DIM)
    mixed = jnp.einsum('gij,bnjgc->bnigc', sgu_w.astype(v.dtype), vc) + sgu_b.astype(v.dtype).T[None, None, :, :, None]
    return u * mixed.reshape(b, L, SGU_WIDTH)


def ssd_sgu_mixer(h_ctx, h_lat, w_in, conv_w, conv_b, dt_bias, a_log, d_skip, ssd_norm_w, sgu_w, sgu_b, w_out, need_ctx):
    p_lat = h_lat @ w_in
    if need_ctx:
        p_ctx = h_ctx @ w_in
        ctx_ssd_cols = p_ctx[..., Z_END:DT_END]
    else:
        p_ctx = None
        ctx_ssd_cols = h_ctx @ w_in[:, Z_END:DT_END]
    lat_in = ssd_inputs(p_lat[..., Z_END:DT_END], conv_w, conv_b, dt_bias)
    ctx_in = ssd_inputs(ctx_ssd_cols, conv_w, conv_b, dt_bias)
    y_ctx, y_lat = ssd_bidirectional(ctx_in, lat_in, a_log, need_ctx)

    def merge(p, y, xs):
        y_ssd = ssd_finish(y, xs, p[..., :Z_END], d_skip, ssd_norm_w)
        y_sgu = spatial_gating(p[..., DT_END:U_END], p[..., U_END:], sgu_w, sgu_b)
        return jnp.concatenate([y_ssd, y_sgu], axis=-1) @ w_out

    o_lat = merge(p_lat, y_lat, lat_in[0])
    o_ctx = merge(p_ctx, y_ctx, ctx_in[0]) if need_ctx else None
    return o_ctx, o_lat


def banded_window_attention(q, k, v, k_c, v_c, sink_kg):
    b, S, H, Dh = q.shape
    T = k_c.shape[1]
    nb = S // ATT_BLOCK
    nw = 3 * ATT_BLOCK
    qb = q.reshape(b, nb, ATT_BLOCK, ATT_KV_HEADS, ATT_GROUP, Dh)
    pad = ((0, 0), (ATT_BLOCK, ATT_BLOCK), (0, 0), (0, 0))

    def band(t):
        tp = jnp.pad(t, pad).reshape(b, nb + 2, ATT_BLOCK, ATT_KV_HEADS, Dh)
        return jnp.concatenate([tp[:, :-2], tp[:, 1:-1], tp[:, 2:]], axis=2)

    kb, vb = band(k), band(v)
    s_win = jnp.einsum('bnqkgd,bnskd->bnkgqs', qb, kb).astype(jnp.float32)
    qpos = (jnp.arange(nb)[:, None] * ATT_BLOCK + jnp.arange(ATT_BLOCK)[None, :])[:, :, None]
    kpos = ((jnp.arange(nb)[:, None] - 1) * ATT_BLOCK + jnp.arange(nw)[None, :])[:, None, :]
    valid = (jnp.abs(qpos - kpos) <= WINDOW) & (kpos >= 0) & (kpos < S)
    s_win = jnp.where(valid[None, :, None, None], s_win, NEG_INF)
    s_ctx = jnp.einsum('bnqkgd,btkd->bnkgqt', qb, k_c).astype(jnp.float32)
    sink = jnp.broadcast_to(sink_kg[None, None, :, :, None, None], s_win.shape[:-1] + (1,))
    p = jax.nn.softmax(jnp.concatenate([s_win, s_ctx, sink], axis=-1), axis=-1).astype(v.dtype)
    o = (jnp.einsum('bnkgqs,bnskd->bnqkgd', p[..., :nw], vb)
         + jnp.einsum('bnkgqt,btkd->bnqkgd', p[..., nw:nw + T], v_c))
    return o.reshape(b, S, H * Dh)


def context_attention(q_c, k_c, v_c, sink_kg):
    b, T = q_c.shape[:2]
    s = jnp.einsum('btkgd,bukd->bkgtu', q_c, k_c).astype(jnp.float32)
    sink = jnp.broadcast_to(sink_kg[None, :, :, None, None], s.shape[:-1] + (1,))
    p = jax.nn.softmax(jnp.concatenate([s, sink], axis=-1), axis=-1)[..., :-1].astype(v_c.dtype)
    return jnp.einsum('bkgtu,bukd->btkgd', p, v_c).reshape(b, T, ATT_HEADS * ATT_HEAD_DIM)


def window_gqa_mixer(h_ctx, h_lat, w_qkv, sink, w_out, cos, sin, need_ctx):
    b, S, _ = h_lat.shape
    T = h_ctx.shape[1]
    scale = ATT_HEAD_DIM ** -0.5
    p_lat = h_lat @ w_qkv
    q_l = axial_rope(p_lat[..., :Q_COLS].reshape(b, S, ATT_HEADS, ATT_HEAD_DIM), cos, sin) * scale
    k_l = axial_rope(p_lat[..., Q_COLS:Q_COLS + KV_COLS].reshape(b, S, ATT_KV_HEADS, ATT_HEAD_DIM), cos, sin)
    v_l = p_lat[..., Q_COLS + KV_COLS:].reshape(b, S, ATT_KV_HEADS, ATT_HEAD_DIM)
    p_ctx = h_ctx @ w_qkv if need_ctx else h_ctx @ w_qkv[:, Q_COLS:]
    k_c = p_ctx[..., -2 * KV_COLS:-KV_COLS].reshape(b, T, ATT_KV_HEADS, ATT_HEAD_DIM)
    v_c = p_ctx[..., -KV_COLS:].reshape(b, T, ATT_KV_HEADS, ATT_HEAD_DIM)
    sink_kg = sink.astype(jnp.float32).reshape(ATT_KV_HEADS, ATT_GROUP)
    o_lat = banded_window_attention(q_l, k_l, v_l, k_c, v_c, sink_kg) @ w_out
    o_ctx = None
    if need_ctx:
        q_c = p_ctx[..., :Q_COLS].reshape(b, T, ATT_KV_HEADS, ATT_GROUP, ATT_HEAD_DIM) * scale
        o_ctx = context_attention(q_c, k_c, v_c, sink_kg) @ w_out
    return o_ctx, o_lat


def setup_inputs(seed: int = 0) -> dict:
    key = jax.random.key(seed)
    ks = jax.random.split(key, 24)
    D = D_MODEL

    def nrm(k, shape, scale):
        return jax.random.normal(k, shape, jnp.float32) * scale

    dt0 = jnp.exp(jax.random.uniform(ks[12], (N_EVEN, 2, SSD_HEADS), jnp.float32,
                                     minval=math.log(1e-3), maxval=math.log(1e-1)))
    return {
        'x': nrm(ks[0], (BATCH, SEQ, D), 1.0),
        'c': nrm(ks[1], (BATCH, D), 1.0),
        'ctx': nrm(ks[2], (BATCH, CTX_LEN, D), 1.0),
        'c_ctx': nrm(ks[3], (D,), 1.0),
        'w_mod': nrm(ks[4], (DEPTH, D, 6 * D), 0.5 * D ** -0.5),
        'b_mod': nrm(ks[5], (DEPTH, 6 * D), 0.02),
        'norm_w': 1.0 + nrm(ks[6], (DEPTH, 4, D), 0.05),
        'w_ffn_in': nrm(ks[7], (DEPTH, D, 2 * FFN_HIDDEN), D ** -0.5),
        'w_ffn_out': nrm(ks[8], (DEPTH, FFN_HIDDEN, D), FFN_HIDDEN ** -0.5),
        'e_w_in': nrm(ks[9], (N_EVEN, D, EVEN_IN_COLS), D ** -0.5),
        'e_conv_w': nrm(ks[10], (N_EVEN, CONV_K, XBC_DIM), CONV_K ** -0.5),
        'e_conv_b': nrm(ks[11], (N_EVEN, XBC_DIM), 0.02),
        'e_dt_bias': dt0 + jnp.log(-jnp.expm1(-dt0)),
        'e_a_log': jnp.log(jax.random.uniform(ks[13], (N_EVEN, 2, SSD_HEADS), jnp.float32, minval=1.0, maxval=16.0)),
        'e_d_skip': 1.0 + nrm(ks[14], (N_EVEN, SSD_HEADS), 0.05),
        'e_ssd_norm_w': 1.0 + nrm(ks[15], (N_EVEN, SSD_INNER), 0.05),
        'e_sgu_w': nrm(ks[16], (N_EVEN, SGU_GROUPS, SGU_CHUNK, SGU_CHUNK), SGU_CHUNK ** -0.5),
        'e_sgu_b': 1.0 + nrm(ks[17], (N_EVEN, SGU_GROUPS, SGU_CHUNK), 0.05),
        'e_w_out': nrm(ks[18], (N_EVEN, MIX_WIDTH, D), MIX_WIDTH ** -0.5),
        'o_w_qkv': nrm(ks[19], (N_ODD, D, QKV_COLS), D ** -0.5),
        'o_sink': nrm(ks[20], (N_ODD, ATT_HEADS), 0.5),
        'o_w_out': nrm(ks[21], (N_ODD, Q_COLS, D), Q_COLS ** -0.5),
    }


def reference(x, c, ctx, c_ctx, w_mod, b_mod, norm_w, w_ffn_in, w_ffn_out, e_w_in, e_conv_w, e_conv_b,
              e_dt_bias, e_a_log, e_d_skip, e_ssd_norm_w, e_sgu_w, e_sgu_b, e_w_out, o_w_qkv, o_sink, o_w_out):
    S = x.shape[1]
    cos, sin = axial_rope_tables(S)
    s_c = jax.nn.silu(c)
    s_cc = jax.nn.silu(c_ctx)
    for l in range(DEPTH):
        need_ctx = l < DEPTH - 1
        mod_l = (s_c @ w_mod[l] + b_mod[l])[:, None, :]
        mod_c = s_cc @ w_mod[l] + b_mod[l]
        sh1, sc1, g1, sh2, sc2, g2 = jnp.split(mod_l, 6, axis=-1)
        csh1, csc1, cg1, csh2, csc2, cg2 = jnp.split(mod_c, 6, axis=-1)
        h_lat = rms_norm(x, norm_w[l, 0]) * (1.0 + sc1) + sh1
        h_ctx = rms_norm(ctx, norm_w[l, 0]) * (1.0 + csc1) + csh1
        if l % 2 == 0:
            i = l // 2
            o_ctx, o_lat = ssd_sgu_mixer(h_ctx, h_lat, e_w_in[i], e_conv_w[i], e_conv_b[i], e_dt_bias[i], e_a_log[i],
                                         e_d_skip[i], e_ssd_norm_w[i], e_sgu_w[i], e_sgu_b[i], e_w_out[i], need_ctx)
        else:
            i = l // 2
            o_ctx, o_lat = window_gqa_mixer(h_ctx, h_lat, o_w_qkv[i], o_sink[i], o_w_out[i], cos, sin, need_ctx)
        x = x + g1 * rms_norm(o_lat, norm_w[l, 1])
        f_lat = swiglu(rms_norm(x, norm_w[l, 2]) * (1.0 + sc2) + sh2, w_ffn_in[l], w_ffn_out[l])
        x = x + g2 * rms_norm(f_lat, norm_w[l, 3])
        if need_ctx:
            ctx = ctx + cg1 * rms_norm(o_ctx, norm_w[l, 1])
            f_ctx = swiglu(rms_norm(ctx, norm_w[l, 2]) * (1.0 + csc2) + csh2, w_ffn_in[l], w_ffn_out[l])
            ctx = ctx + cg2 * rms_norm(f_ctx, norm_w[l, 3])
    return x
```

```python
import math
from contextlib import ExitStack
import numpy as np
import concourse.bass as bass
import concourse.mybir as mybir
from concourse.bass_utils import run_bass_kernel_spmd

F32 = mybir.dt.float32
BF16 = mybir.dt.bfloat16
AF = mybir.ActivationFunctionType
ALU = mybir.AluOpType
AX = mybir.AxisListType
EPS = 1e-6


class Cfg:
    def __init__(self, D=2048, S=4096, CTX=256, DEPTH=4, GRID_W=64, TT=2048, TF=1024):
        self.D, self.S, self.CTX, self.DEPTH, self.GRID_W = D, S, CTX, DEPTH, GRID_W
        self.KC = D // 128
        self.NT = CTX + S
        self.INNER = D // 2
        self.H = self.INNER // 64
        self.HPG = self.H // 2
        self.XBC = self.INNER + 512
        self.XBCC = self.XBC // 128
        self.SW = D - self.INNER
        self.SG = self.SW // 128
        self.Z_END = self.INNER
        self.XBC_END = self.Z_END + self.XBC
        self.DT_END = self.XBC_END + 2 * self.H
        self.U_END = self.DT_END + self.SW
        self.EIN = self.U_END + self.SW
        self.AH = D // 128
        self.KVH = self.AH // 4
        self.QC = D
        self.KVC = self.KVH * 128
        self.QKV = self.QC + 2 * self.KVC
        self.FH = -(-(8 * D) // (3 * 256)) * 256
        self.NE = (DEPTH + 1) // 2
        self.NO = DEPTH // 2
        self.TT = min(TT, S)
        self.TF = min(TF, S)
        o = 0
        self.o_c = o; o += self.KC * 2
        self.o_bmod = o; o += DEPTH * 6 * self.KC
        self.o_nw = o; o += DEPTH * 4 * self.KC
        self.o_cw = o; o += self.NE * self.XBCC * 5
        self.o_cb = o; o += self.NE * self.XBCC
        self.NPP = o
        o = 0
        self.b_dtb = o; o += self.NE * 2 * self.H
        self.b_alog = o; o += self.NE * 2 * self.H
        self.b_dsk = o; o += self.NE * self.H
        self.b_snw = o; o += self.NE * self.INNER
        self.b_sgub = o; o += self.NE * self.SW
        self.b_sink = o; o += max(self.NO, 1) * self.AH
        self.NPB = o


ENGS = ["pe", "act", "dve", "pool", "sp"]
NSLOT = 6


class Buf:
    __slots__ = ("w", "r")

    def __init__(self):
        self.w = None
        self.r = []


class Sched:
    def __init__(self):
        self.st = {e: [] for e in ENGS}
        self.seen = {e: {} for e in ENGS}
        self.dma_use = {}
        self.dma_rr = {"sp": 0, "pool": 0}
        self.n_ops = 0

    def _finish(self, eng, item, deps, tok, reads, writes):
        waits = []
        seen = self.seen[eng]
        for d in deps:
            if d[0] == "c":
                _, e2, i = d
                if eng == "pe" and e2 == "pe":
                    continue
                if seen.get(("c", e2), -1) >= i:
                    continue
                seen[("c", e2)] = i
                self.st[e2][i]["ms"] = True
                waits.append(d)
            else:
                _, q, slot, use = d
                if seen.get(("d", q, slot), -1) >= use:
                    continue
                seen[("d", q, slot)] = use
                waits.append(d)
        item["waits"] = waits
        self.st[eng].append(item)
        for b in reads:
            b.r.append(tok)
        for b in writes:
            b.w = tok
            b.r = []
        self.n_ops += 1

    @staticmethod
    def _deps(reads, writes):
        deps = set()
        for b in reads:
            if b.w is not None:
                deps.add(b.w)
        for b in writes:
            if b.w is not None:
                deps.add(b.w)
            deps.update(b.r)
        return deps

    def op(self, eng, fn, reads=(), writes=()):
        item = {"fn": fn, "ms": False, "dma": None}
        tok = ("c", eng, len(self.st[eng]))
        self._finish(eng, item, self._deps(reads, writes), tok, reads, writes)

    def dma(self, q, fn, reads=(), writes=()):
        slot = self.dma_rr[q]
        self.dma_rr[q] = (slot + 1) % NSLOT
        use = self.dma_use.get((q, slot), 0)
        self.dma_use[(q, slot)] = use + 1
        tok = ("d", q, slot, use)
        deps = self._deps(reads, writes)
        if use > 0:
            deps.add(("d", q, slot, use - 1))
        item = {"fn": fn, "ms": False, "dma": (q, slot)}
        self._finish(q, item, deps, tok, reads, writes)

    def barrier(self):
        toks = []
        for e in ENGS:
            for i in range(len(self.st[e]) - 1, -1, -1):
                it = self.st[e][i]
                if it["fn"] is not None and it["dma"] is None:
                    toks.append(("c", e, i))
                    break
        for (q, slot), use in self.dma_use.items():
            toks.append(("d", q, slot, use - 1))
        for e in ENGS:
            item = {"fn": None, "ms": False, "dma": None}
            self._finish(e, item, set(toks), None, (), ())

    def emit(self, block, sems):
        msc = {}
        for e, items in self.st.items():
            c = 0
            arr = []
            for it in items:
                if it["ms"]:
                    c += 1
                arr.append(c)
            msc[e] = arr

        def run(e, eo):
            for it in self.st[e]:
                for w in it["waits"]:
                    if w[0] == "c":
                        eo.wait_ge(sems[("c", w[1])], msc[w[1]][w[2]])
                    else:
                        eo.wait_ge(sems[("d", w[1], w[2])], 16 * (w[3] + 1))
                if it["fn"] is not None:
                    ins = it["fn"](eo)
                    if it["dma"] is not None:
                        ins.then_inc(sems[("d",) + it["dma"]], 16)
                    elif it["ms"]:
                        ins.then_inc(sems[("c", e)], 1)

        @block.tensor
        def _(t):
            run("pe", t)

        @block.scalar
        def _(a):
            run("act", a)

        @block.vector
        def _(v):
            run("dve", v)

        @block.gpsimd
        def _(g):
            run("pool", g)

        @block.sync
        def _(s):
            run("sp", s)


class Tl:
    __slots__ = ("ap", "b")

    def __init__(self, ap):
        self.ap = ap
        self.b = Buf()


class Arena:
    def __init__(self, ap32, nwords):
        self.ap32 = ap32
        self.n = nwords
        self.off = 0

    def reset(self):
        self.off = 0

    def f32(self, n):
        assert self.off + n <= self.n, ("arena overflow", self.off, n, self.n)
        v = self.ap32[:, self.off:self.off + n]
        self.off += n
        return Tl(v)

    def bf(self, n):
        w = (n + 1) // 2
        assert self.off + w <= self.n, ("arena overflow", self.off, w, self.n)
        v = self.ap32[:, self.off:self.off + w].bitcast(BF16)
        self.off += w
        return Tl(v)


class _Stop(Exception):
    pass


class Builder:
    def __init__(self, cfg, dbg=(), stop_at=None):
        self.c = cfg
        self.dbg = set(dbg)
        self.s = Sched()
        self.stop_at = stop_at
        self.pcount = 0

    def dma(self, out, in_, reads=(), writes=(), q="sp"):
        def fn(e):
            return e.dma_start(out=out, in_=in_)
        self.s.dma(q, fn, reads, writes)

    def act(self, out, in_, func, reads, writes, bias=None, scale=None, accum=None):
        kw = {}
        if bias is not None:
            kw["bias"] = bias
        if scale is not None:
            kw["scale"] = scale
        if accum is not None:
            kw["accum_out"] = accum

        def fn(e):
            return e.activation(out=out, in_=in_, func=func, **kw)
        self.s.op("act", fn, reads, writes)

    def tt(self, out, in0, in1, op, reads, writes, eng="dve"):
        def fn(e):
            return e.tensor_tensor(out=out, in0=in0, in1=in1, op=op)
        self.s.op(eng, fn, reads, writes)

    def ts(self, out, in0, s1, op0, reads, writes, s2=None, op1=None, eng="dve"):
        def fn(e):
            if op1 is None:
                return e.tensor_scalar(out=out, in0=in0, scalar1=s1, scalar2=None, op0=op0)
            return e.tensor_scalar(out=out, in0=in0, scalar1=s1, scalar2=s2, op0=op0, op1=op1)
        self.s.op(eng, fn, reads, writes)

    def stt(self, out, in0, scalar, in1, op0, op1, reads, writes, eng="dve"):
        def fn(e):
            return e.scalar_tensor_tensor(out=out, in0=in0, scalar=scalar, in1=in1, op0=op0, op1=op1)
        self.s.op(eng, fn, reads, writes)

    def cp(self, out, in_, reads, writes, eng="dve"):
        if eng == "act":
            def fn(e):
                return e.copy(out=out, in_=in_)
        else:
            def fn(e):
                return e.tensor_copy(out=out, in_=in_)
        self.s.op(eng, fn, reads, writes)

    def recip(self, out, in_, reads, writes):
        def fn(e):
            return e.reciprocal(out=out, in_=in_)
        self.s.op("dve", fn, reads, writes)

    def memset(self, out, val, writes, eng="dve"):
        def fn(e):
            return e.memset(out, val)
        self.s.op(eng, fn, (), writes)

    def mm(self, specs, reads, writes):
        def fn(e):
            ins = None
            for (o, l, r, st, sp) in specs:
                ins = e.matmul(o, lhsT=l, rhs=r, start=st, stop=sp)
            return ins
        self.s.op("pe", fn, reads, writes)

    def tr(self, specs, reads, writes):
        ident = self.ident_bf.ap
        def fn(e):
            ins = None
            for (o, i) in specs:
                ins = e.transpose(o, i, ident)
            return ins
        self.s.op("pe", fn, list(reads) + [self.ident_bf.b], writes)

    def phase(self):
        self.pcount += 1
        if self.stop_at is not None and self.pcount > self.stop_at:
            raise _Stop()
        self.s.barrier()
        self.ar.reset()

    def rsqrt_from_ss(self, R, ss_ps, T, scale, tmp):
        self.ts(tmp.ap[:, :T], ss_ps.ap[:, :T], scale, ALU.mult, [ss_ps.b], [tmp.b], s2=EPS, op1=ALU.add)
        self.act(tmp.ap[:, :T], tmp.ap[:, :T], AF.Sqrt, [tmp.b], [tmp.b])
        self.recip(R.ap[:, :T], tmp.ap[:, :T], [tmp.b], [R.b])

    def build(self):
        c = self.c
        nc = bass.Bass("TRN2", target_bir_lowering=False)
        self.nc = nc
        D, NT, S, CTX = c.D, c.NT, c.S, c.CTX

        def din(name, shape, dt=F32):
            return nc.dram_tensor(name, list(shape), dt, kind="ExternalInput")

        self.i_xT = din("xT", [D, NT])
        self.i_pp = din("pp", [128, c.NPP])
        self.i_pb = din("pb", [128, c.NPB])
        self.i_cst = din("cst", [128, 7 * 128])
        self.i_rope = din("rope", [128, 2, S])
        self.i_wmod = din("w_mod", [c.DEPTH, D, 6 * D])
        self.i_wfi = din("w_ffn_in", [c.DEPTH, D, 2 * c.FH])
        self.i_wfo = din("w_ffn_out", [c.DEPTH, c.FH, D])
        self.i_ewin = din("e_w_in", [c.NE, D, c.EIN])
        self.i_ewout = din("e_w_out", [c.NE, D, D])
        self.i_sguw = din("sguwT", [c.NE, 128, c.SW])
        self.i_oqkv = din("o_w_qkv", [max(c.NO, 1), D, c.QKV])
        self.i_owout = din("o_w_out", [max(c.NO, 1), D, D])
        self.o_y = nc.dram_tensor("y", [D, S], F32, kind="ExternalOutput")

        def scr(name, shape, dt):
            kind = "ExternalOutput" if name in self.dbg else "Internal"
            return nc.dram_tensor(name, list(shape), dt, kind=kind)

        self.xT = scr("s_xT", [D, NT], F32)
        self.hT = scr("s_hT", [D, NT], BF16)
        self.oT = scr("s_oT", [D, NT], F32)
        self.hidT = scr("s_hidT", [c.FH, NT], BF16)
        self.pxbcT = scr("s_pxbcT", [c.XBC, NT], BF16)
        self.xbcT = scr("s_xbcT", [c.XBC, NT], BF16)
        self.z_tm = scr("s_z", [NT, c.INNER], BF16)
        self.dt_tm = scr("s_dt", [NT, 2 * c.H], F32)
        self.uT = scr("s_uT", [c.SW, NT], BF16)
        self.v_tm = scr("s_v", [NT, c.SW], BF16)
        self.yf_tm = scr("s_yf", [NT, c.INNER], F32)
        self.mixT = scr("s_mixT", [D, NT], BF16)
        self.qT = scr("s_qT", [c.QC, NT], BF16)
        self.kT = scr("s_kT", [c.KVC, NT], BF16)
        self.va_tm = scr("s_va", [NT, c.KVC], BF16)
        self.dbgmods = scr("s_mods", [128, c.DEPTH * 6 * c.KC * 2], F32)

        ARW = 32768
        with ExitStack() as es:
            ent = es.enter_context
            arena_t = ent(nc.sbuf_tensor("arena", [128, ARW], F32))
            cst32 = ent(nc.sbuf_tensor("cst32", [128, 7 * 128], F32))
            cstbf = ent(nc.sbuf_tensor("cstbf", [128, 7 * 128], BF16))
            pp = ent(nc.sbuf_tensor("ppt", [128, c.NPP], F32))
            pb = ent(nc.sbuf_tensor("pbt", [128, c.NPB], F32))
            mods = ent(nc.sbuf_tensor("mods", [128, c.DEPTH * 6 * c.KC * 2], F32))
            sc = ent(nc.sbuf_tensor("scal", [128, c.DEPTH * 6 * c.KC * 2], F32))
            psum = ent(nc.psum_tensor("psum", [128, 8, 512], F32))
            sems = {}
            for e in ["pe", "act", "dve", "pool"]:
                sems[("c", e)] = ent(nc.semaphore("c_" + e))
            for q in ["sp", "pool"]:
                for i in range(NSLOT):
                    sems[("d", q, i)] = ent(nc.semaphore("d_%s%d" % (q, i)))
            block = ent(nc.Block())

            self.ar = Arena(arena_t[:, :], ARW)
            self.cst32 = Tl(cst32[:, :])
            self.cstbf = Tl(cstbf[:, :])
            self.pp = Tl(pp[:, :])
            self.pb = Tl(pb[:, :])
            self.mods = Tl(mods[:, :])
            self.sc = Tl(sc[:, :])
            self.bank = [Tl(psum[:, i, :]) for i in range(8)]
            self.ident_bf = Tl(cstbf[:, 4 * 128:5 * 128])
            self.ident_bf.b = self.cstbf.b

            try:
                self.prologue()
                self.p_mod()
                self.p_postnorm(0, post=None, norm=0)
                for l in range(c.DEPTH):
                    if l % 2 == 0:
                        self.even_mixer(l)
                    else:
                        self.odd_mixer(l)
                    self.p_postnorm(l, post=1, norm=2)
                    self.p_ffn_in(l)
                    self.p_ffn_out(l)
                    last = (l == c.DEPTH - 1)
                    self.p_postnorm(l, post=3, norm=None if last else 0, final=last)
            except _Stop:
                pass
            self.s.barrier()
            self.s.emit(block, sems)
        return nc

    def m32(self, i):
        return self.cst32.ap[:, i * 128:(i + 1) * 128]

    def mbf(self, i):
        return self.cstbf.ap[:, i * 128:(i + 1) * 128]

    def prologue(self):
        c = self
        self.dma(self.cst32.ap, self.i_cst[:, :], (), [self.cst32.b])
        self.dma(self.pp.ap, self.i_pp[:, :], (), [self.pp.b])
        self.dma(self.pb.ap, self.i_pb[:, :], (), [self.pb.b])
        self.cp(self.cstbf.ap, self.cst32.ap, [self.cst32.b], [self.cstbf.b])

    def p_mod(self):
        c = self.c
        KC = c.KC
        self.phase()
        sT = self.ar.f32(KC * 2)
        self.act(sT.ap, self.pp.ap[:, c.o_c:c.o_c + KC * 2], AF.Silu, [self.pp.b], [sT.b])
        WB = 512
        wt = [self.ar.f32(KC * WB) for _ in range(2)]
        nblk = 6 * c.D // WB
        it = 0
        mods4 = self.mods.ap.rearrange("p (l j t) -> p l j t", l=c.DEPTH, t=2)
        for l in range(c.DEPTH):
            for wb in range(nblk):
                w = wt[it % 2]
                bk = self.bank[it % 2]
                it += 1
                src = self.i_wmod[l, :, wb * WB:(wb + 1) * WB].rearrange("(k p) n -> p k n", p=128)
                self.dma(w.ap.rearrange("p (k n) -> p k n", n=WB), src, (), [w.b])
                w3 = w.ap.rearrange("p (k n) -> p k n", n=WB)
                s3 = sT.ap.rearrange("p (k t) -> p k t", t=2)
                specs = []
                for m in range(WB // 128):
                    for k in range(KC):
                        specs.append((bk.ap[:, m * 2:m * 2 + 2], w3[:, k, m * 128:(m + 1) * 128], s3[:, k, :],
                                      k == 0, k == KC - 1))
                self.mm(specs, [w.b, sT.b], [bk.b])
                for m in range(WB // 128):
                    j = wb * (WB // 128) + m
                    bcol = c.o_bmod + l * 6 * KC + j
                    self.ts(mods4[:, l, j, :], bk.ap[:, m * 2:m * 2 + 2], self.pp.ap[:, bcol:bcol + 1], ALU.add,
                            [bk.b, self.pp.b], [self.mods.b])
        sc5 = self.sc.ap.rearrange("p (l w k t) -> p l w k t", l=c.DEPTH, w=6, t=2)
        md5 = self.mods.ap.rearrange("p (l w k t) -> p l w k t", l=c.DEPTH, w=6, t=2)
        for l in range(c.DEPTH):
            nw = self.pp.ap[:, c.o_nw + l * 4 * KC:c.o_nw + (l + 1) * 4 * KC].rearrange("p (i k) -> p i k", k=KC)
            for t in range(2):
                for (dst, scw, nwi) in ((0, 1, 0), (3, 4, 2)):
                    self.stt(sc5[:, l, dst, :, t], md5[:, l, scw, :, t], 1.0, nw[:, nwi, :], ALU.add, ALU.mult,
                             [self.mods.b, self.pp.b], [self.sc.b])
                for (dst, src) in ((1, 0), (4, 3)):
                    self.cp(sc5[:, l, dst, :, t], md5[:, l, src, :, t], [self.mods.b], [self.sc.b])
                for (dst, gw, nwi) in ((2, 2, 1), (5, 5, 3)):
                    self.tt(sc5[:, l, dst, :, t], md5[:, l, gw, :, t], nw[:, nwi, :], ALU.mult,
                            [self.mods.b, self.pp.b], [self.sc.b])
        if "s_mods" in self.dbg:
            self.dma(self.dbgmods[:, :], self.mods.ap, [self.mods.b], ())

    def scal(self, l, which, kc, t):
        c = self.c
        idx = ((l * 6 + which) * c.KC + kc) * 2 + t
        return self.sc.ap[:, idx:idx + 1]

    def tok_tiles(self, T):
        c = self.c
        tiles = [(0, c.CTX, 1)]
        t0 = c.CTX
        while t0 < c.NT:
            n = min(T, c.NT - t0)
            tiles.append((t0, n, 0))
            t0 += n
        return tiles

    def p_postnorm(self, l, post, norm, final=False):
        c = self.c
        KC, D = c.KC, c.D
        self.phase()
        T = 512
        X = self.ar.f32(KC * T)
        O = self.ar.f32(KC * T)
        SQ = self.ar.bf(KC * T)
        Hh = self.ar.bf(KC * T)
        R = self.ar.f32(T)
        tmp = self.ar.f32(T)
        tm2 = [self.ar.f32(T) for _ in range(2)]
        ones = self.mbf(5)
        nl = l + 1 if (post == 3) else l
        for (t0, n, isctx) in self.tok_tiles(T):
            X3 = X.ap.rearrange("p (k t) -> p k t", t=T)
            O3 = O.ap.rearrange("p (k t) -> p k t", t=T)
            S3 = SQ.ap.rearrange("p (k t) -> p k t", t=T)
            H3 = Hh.ap.rearrange("p (k t) -> p k t", t=T)
            xsrc = self.xT if post is not None else self.i_xT
            self.dma(X3[:, :, :n], xsrc[:, t0:t0 + n].rearrange("(k p) t -> p k t", p=128), (), [X.b])
            if post is None:
                self.dma(self.xT[:, t0:t0 + n].rearrange("(k p) t -> p k t", p=128), X3[:, :, :n], [X.b], ())
            if post is not None:
                self.dma(O3[:, :, :n], self.oT[:, t0:t0 + n].rearrange("(k p) t -> p k t", p=128), (), [O.b])
                gw = 2 if post == 1 else 5
                for k in range(KC):
                    self.act(S3[:, k, :n], O3[:, k, :n], AF.Square, [O.b], [SQ.b])
                bk = self.bank[0]
                self.mm([(bk.ap[:, :n], ones, S3[:, k, :n], k == 0, k == KC - 1) for k in range(KC)],
                        [SQ.b, self.cstbf.b], [bk.b])
                self.rsqrt_from_ss(R, bk, n, 1.0 / D, tmp)
                for k in range(KC):
                    tk = tm2[k % 2]
                    self.stt(tk.ap[:, :n], O3[:, k, :n], self.scal(l, gw, k, isctx), R.ap[:, :n], ALU.mult, ALU.mult,
                             [O.b, R.b, self.sc.b], [tk.b])
                    self.tt(X3[:, k, :n], X3[:, k, :n], tk.ap[:, :n], ALU.add, [X.b, tk.b], [X.b], eng="pool")
                self.dma(self.xT[:, t0:t0 + n].rearrange("(k p) t -> p k t", p=128), X3[:, :, :n], [X.b], ())
                if final and not isctx:
                    self.dma(self.o_y[:, t0 - c.CTX:t0 - c.CTX + n].rearrange("(k p) t -> p k t", p=128),
                             X3[:, :, :n], [X.b], ())
            if norm is not None:
                aw, bw = (0, 1) if norm == 0 else (3, 4)
                for k in range(KC):
                    self.act(S3[:, k, :n], X3[:, k, :n], AF.Square, [X.b], [SQ.b])
                bk = self.bank[1]
                self.mm([(bk.ap[:, :n], ones, S3[:, k, :n], k == 0, k == KC - 1) for k in range(KC)],
                        [SQ.b, self.cstbf.b], [bk.b])
                self.rsqrt_from_ss(R, bk, n, 1.0 / D, tmp)
                for k in range(KC):
                    tk = tm2[k % 2]
                    self.stt(tk.ap[:, :n], X3[:, k, :n], self.scal(nl, aw, k, isctx), R.ap[:, :n], ALU.mult, ALU.mult,
                             [X.b, R.b, self.sc.b], [tk.b])
                    self.act(H3[:, k, :n], tk.ap[:, :n], AF.Identity, [tk.b, self.sc.b], [Hh.b],
                             bias=self.scal(nl, bw, k, isctx), scale=1.0)
                self.dma(self.hT[:, t0:t0 + n].rearrange("(k p) t -> p k t", p=128), H3[:, :, :n], [Hh.b], ())

    def p_linear(self, xT, K, W, groups, T):
        c = self.c
        KCH = K // 128
        self.phase()
        WB = 512 if KCH <= 16 else 128
        XA = self.ar.bf(KCH * T)
        Wt = [self.ar.bf(KCH * WB) for _ in range(2)]
        OF = [self.ar.f32(min(T, 2048)) for _ in range(3)]
        X3 = XA.ap.rearrange("p (k t) -> p k t", t=T)
        wi = 0
        oi = 0
        bi = 0
        for (t0, n, isctx) in self.tok_tiles(T):
            self.dma(X3[:, :, :n], xT[:, t0:t0 + n].rearrange("(k p) t -> p k t", p=128), (), [XA.b])
            nsub = (n + 511) // 512
            for (c0, c1, mode, dst, ddt) in groups:
                cb = c0
                while cb < c1:
                    wc = min(WB, c1 - cb)
                    w = Wt[wi % 2]
                    wi += 1
                    W3 = w.ap.rearrange("p (k n) -> p k n", n=WB)
                    self.dma(W3[:, :, :wc], W[:, cb:cb + wc].rearrange("(k p) n -> p k n", p=128), (), [w.b], q="pool")
                    if mode == "fm":
                        for m in range(wc // 128):
                            if bi + nsub > 8:
                                bi = 0
                            bks = [self.bank[bi + j] for j in range(nsub)]
                            bi += nsub
                            specs = []
                            for k in range(KCH):
                                for j in range(nsub):
                                    tj = min(512, n - j * 512)
                                    specs.append((bks[j].ap[:, :tj], W3[:, k, m * 128:(m + 1) * 128],
                                                  X3[:, k, j * 512:j * 512 + tj], k == 0, k == KCH - 1))
                            self.mm(specs, [w.b, XA.b], [b.b for b in bks])
                            of = OF[oi % 3]
                            oi += 1
                            ov = of.ap if ddt == F32 else of.ap.bitcast(BF16)
                            for j in range(nsub):
                                tj = min(512, n - j * 512)
                                eng = "act" if (j % 2 == 0) else "dve"
                                self.cp(ov[:, j * 512:j * 512 + tj], bks[j].ap[:, :tj], [bks[j].b], [of.b], eng=eng)
                            r0 = cb - c0 + m * 128
                            self.dma(dst[r0:r0 + 128, t0:t0 + n], ov[:, :n], [of.b], ())
                    else:
                        for tb in range(n // 128):
                            if bi + 1 > 8:
                                bi = 0
                            bk = self.bank[bi]
                            bi += 1
                            specs = [(bk.ap[:, :wc], X3[:, k, tb * 128:(tb + 1) * 128], W3[:, k, :wc],
                                      k == 0, k == KCH - 1) for k in range(KCH)]
                            self.mm(specs, [w.b, XA.b], [bk.b])
                            of = OF[oi % 3]
                            oi += 1
                            ov = of.ap if ddt == F32 else of.ap.bitcast(BF16)
                            self.cp(ov[:, :wc], bk.ap[:, :wc], [bk.b], [of.b], eng="act" if tb % 2 == 0 else "dve")
                            self.dma(dst[t0 + tb * 128:t0 + (tb + 1) * 128, cb - c0:cb - c0 + wc], ov[:, :wc], [of.b], ())
                    cb += wc

    def p_ffn_in(self, l):
        c = self.c
        KC, FH, T = c.KC, c.FH, c.TF
        self.phase()
        WB = 256
        XA = self.ar.bf(KC * T)
        Wt = [self.ar.bf(KC * 2 * WB) for _ in range(2)]
        SI = [self.ar.f32(512) for _ in range(2)]
        HO = [self.ar.bf(T) for _ in range(3)]
        X3 = XA.ap.rearrange("p (k t) -> p k t", t=T)
        W = self.i_wfi
        wi = oi = si = 0
        bi = 0
        for (t0, n, isctx) in self.tok_tiles(T):
            self.dma(X3[:, :, :n], self.hT[:, t0:t0 + n].rearrange("(k p) t -> p k t", p=128), (), [XA.b])
            nsub = (n + 511) // 512
            for cb in range(0, FH, WB):
                wc = min(WB, FH - cb)
                w = Wt[wi % 2]
                wi += 1
                W4 = w.ap.rearrange("p (k g n) -> p k g n", g=2, n=WB)
                for g in range(2):
                    self.dma(W4[:, :, g, :wc], W[l, :, g * FH + cb:g * FH + cb + wc].rearrange("(k p) n -> p k n", p=128),
                             (), [w.b], q="pool")
                for m in range(wc // 128):
                    if bi + 2 * nsub > 8:
                        bi = 0
                    gb = [self.bank[bi + j] for j in range(nsub)]
                    ub = [self.bank[bi + nsub + j] for j in range(nsub)]
                    bi += 2 * nsub
                    specs = []
                    for g, bks in ((0, gb), (1, ub)):
                        for k in range(KC):
                            for j in range(nsub):
                                tj = min(512, n - j * 512)
                                specs.append((bks[j].ap[:, :tj], W4[:, k, g, m * 128:(m + 1) * 128],
                                              X3[:, k, j * 512:j * 512 + tj], k == 0, k == KC - 1))
                    self.mm(specs, [w.b, XA.b], [b.b for b in gb + ub])
                    ho = HO[oi % 3]
                    oi += 1
                    for j in range(nsub):
                        tj = min(512, n - j * 512)
                        sg = SI[si % 2]
                        si += 1
                        self.act(sg.ap[:, :tj], gb[j].ap[:, :tj], AF.Silu, [gb[j].b], [sg.b])
                        self.tt(ho.ap[:, j * 512:j * 512 + tj], sg.ap[:, :tj], ub[j].ap[:, :tj], ALU.mult,
                                [sg.b, ub[j].b], [ho.b])
                    r0 = cb + m * 128
                    self.dma(self.hidT[r0:r0 + 128, t0:t0 + n], ho.ap[:, :n], [ho.b], ())

    def p_ffn_out(self, l):
        c = self.c
        self.p_linear(self.hidT, c.FH, self.i_wfo[l, :, :], [(0, c.D, "fm", self.oT, F32)], c.TF)

    def even_mixer(self, l):
        c = self.c
        i = l // 2
        W = self.i_ewin[i, :, :]
        self.p_linear(self.hT, c.D, W, [
            (0, c.Z_END, "tm", self.z_tm, BF16),
            (c.Z_END, c.XBC_END, "fm", self.pxbcT, BF16),
            (c.XBC_END, c.DT_END, "tm", self.dt_tm, F32),
            (c.DT_END, c.U_END, "fm", self.uT, BF16),
            (c.U_END, c.EIN, "tm", self.v_tm, BF16),
        ], c.TT)
        self.p_conv(i)
        self.p_ssd(i, 0)
        self.p_ssd(i, 1)
        self.p_sgu(i)
        self.p_linear(self.mixT, c.D, self.i_ewout[i, :, :], [(0, c.D, "fm", self.oT, F32)], c.TT)

    def p_conv(self, i):
        c = self.c
        self.phase()
        T = min(2048, c.S)
        P = [self.ar.bf(T + 4) for _ in range(2)]
        ACC = [self.ar.f32(T) for _ in range(2)]
        OUT = [self.ar.bf(T) for _ in range(2)]
        it = 0
        segs = [(0, c.CTX)] + [(c.CTX, c.S)]
        for cc in range(c.XBCC):
            for (s0, sl) in segs:
                for t0 in range(0, sl, T):
                    n = min(T, sl - t0)
                    p, acc, out = P[it % 2], ACC[it % 2], OUT[it % 2]
                    eng = "dve"
                    it += 1
                    lo = max(t0 - 2, 0)
                    hi = min(t0 + n + 2, sl)
                    self.memset(p.ap[:, 0:n + 4], 0.0, [p.b], eng="pool")
                    self.dma(p.ap[:, lo - (t0 - 2):hi - (t0 - 2)], self.pxbcT[cc * 128:(cc + 1) * 128, s0 + lo:s0 + hi],
                             (), [p.b])
                    wcol = c.o_cw + (i * c.XBCC + cc) * 5
                    self.ts(acc.ap[:, :n], p.ap[:, 0:n], self.pp.ap[:, wcol:wcol + 1], ALU.mult, [p.b, self.pp.b], [acc.b],
                            eng=eng)
                    for k in range(1, 5):
                        self.stt(acc.ap[:, :n], p.ap[:, k:k + n], self.pp.ap[:, wcol + k:wcol + k + 1], acc.ap[:, :n],
                                 ALU.mult, ALU.add, [p.b, acc.b, self.pp.b], [acc.b], eng=eng)
                    bcol = c.o_cb + i * c.XBCC + cc
                    self.act(out.ap[:, :n], acc.ap[:, :n], AF.Silu, [acc.b, self.pp.b], [out.b],
                             bias=self.pp.ap[:, bcol:bcol + 1], scale=1.0)
                    self.dma(self.xbcT[cc * 128:(cc + 1) * 128, s0 + t0:s0 + t0 + n], out.ap[:, :n], [out.b], ())

    def _cut(self, n):
        import os
        return int(os.environ.get("SSDCUT", "99")) <= n

    def _acut(self, n):
        import os
        return int(os.environ.get("ATTCUT", "99")) <= n

    def p_ssd(self, i, d):
        c = self.c
        H, HPG, INNER, XC = c.H, c.HPG, c.INNER, c.XBCC
        NX = INNER // 128
        GW = HPG * 64
        hb = min(4, HPG)
        self.phase()
        ar = self.ar
        XB = ar.bf(XC * 128)
        DTR = ar.f32(2 * H)
        xx = ar.f32(H); ax = ar.f32(H); ee = ar.f32(H); dtv = ar.f32(H); da = ar.f32(H)
        ea = ar.f32(H); wst = ar.f32(H); dec = ar.f32(H); dtw = ar.f32(H)
        Xb = ar.bf(INNER); Xw = ar.bf(INNER); Btm = ar.bf(256)
        cbm = ar.bf(256)
        rhsD = ar.bf(hb * 128); E = ar.bf(hb * 128); MT = ar.bf(hb * 128)
        Y = ar.f32(INNER); tmpY = ar.f32(INNER)
        S32 = ar.f32(2 * GW); Sbf = ar.bf(2 * GW)
        Zt = ar.bf(INNER); YF = ar.f32(INNER); sg = ar.f32(INNER); ssq = ar.f32(2); rs = ar.f32(2); junk = ar.f32(INNER)
        Yn = ar.bf(INNER); YT = ar.bf(INNER)
        bk = self.bank
        pbb = self.pb
        o_dtb = c.b_dtb + i * 2 * H + d * H
        o_al = c.b_alog + i * 2 * H + d * H
        a_bc = ar.f32(H)
        cs = ar.f32(2 * H)
        self.act(a_bc.ap, pbb.ap[:, o_al:o_al + H], AF.Exp, [pbb.b], [a_bc.b])
        self.ts(a_bc.ap, a_bc.ap, -1.0, ALU.mult, [a_bc.b], [a_bc.b])
        self.memset(S32.ap, 0.0, [S32.b])
        self.memset(Sbf.ap, 0.0, [Sbf.b])
        Mcs = self.m32(0) if d == 0 else self.m32(1)
        Mseg = self.mbf(2) if d == 0 else self.mbf(3)
        Mrhs = self.m32(0) if d == 0 else self.m32(1)
        ones32 = self.m32(5)
        nct, nlt = c.CTX // 128, c.S // 128
        if d == 0:
            chunks = list(range(nct)) + [nct + j for j in range(nlt)]
        else:
            chunks = list(range(nct - 1, -1, -1)) + [nct + j for j in range(nlt - 1, -1, -1)]
        XB3 = XB.ap.rearrange("p (k t) -> p k t", t=128)
        iB, iC = NX, NX + 2
        for ch in chunks:
            t0 = ch * 128
            self.dma(XB3, self.xbcT[:, t0:t0 + 128].rearrange("(k p) t -> p k t", p=128), (), [XB.b])
            self.dma(DTR.ap, self.dt_tm[t0:t0 + 128, :], (), [DTR.b])
            self.tt(xx.ap, DTR.ap[:, d * H:(d + 1) * H], pbb.ap[:, o_dtb:o_dtb + H], ALU.add, [DTR.b, pbb.b], [xx.b])
            self.ts(ee.ap, xx.ap, -1.0, ALU.mult, [xx.b], [ee.b])
            self.tt(ax.ap, xx.ap, ee.ap, ALU.max, [xx.b, ee.b], [ax.b])
            self.act(ee.ap, ax.ap, AF.Exp, [ax.b], [ee.b], scale=-1.0)
            self.ts(ee.ap, ee.ap, 1.0, ALU.add, [ee.b], [ee.b])
            self.act(ee.ap, ee.ap, AF.Ln, [ee.b], [ee.b])
            self.ts(xx.ap, xx.ap, 0.0, ALU.max, [xx.b], [xx.b])
            self.tt(dtv.ap, xx.ap, ee.ap, ALU.add, [xx.b, ee.b], [dtv.b])
            self.tt(da.ap, dtv.ap, a_bc.ap, ALU.mult, [dtv.b, a_bc.b], [da.b])
            if self._cut(1):
                return
            b0 = bk[0]
            self.mm([(b0.ap[:, 0:H], Mcs, da.ap, True, True), (b0.ap[:, H:2 * H], ones32, da.ap, True, True)],
                    [da.b, self.cst32.b], [b0.b])
            self.cp(cs.ap, b0.ap[:, 0:2 * H], [b0.b], [cs.b])
            self.act(ea.ap, cs.ap[:, 0:H], AF.Exp, [cs.b], [ea.b])
            self.tt(wst.ap, cs.ap[:, H:2 * H], cs.ap[:, 0:H], ALU.subtract, [cs.b], [wst.b])
            self.act(wst.ap, wst.ap, AF.Exp, [wst.b], [wst.b])
            self.act(dec.ap, cs.ap[:, H:2 * H], AF.Exp, [cs.b], [dec.b])
            self.tt(dtw.ap, dtv.ap, wst.ap, ALU.mult, [dtv.b, wst.b], [dtw.b])
            if self._cut(2):
                return
            pB = [bk[1].ap.bitcast(BF16), bk[2].ap.bitcast(BF16)]
            specs = []
            for k in range(NX + 2):
                src = XB3[:, k, :] if k < NX else XB3[:, iB + (k - NX), :]
                specs.append((pB[k // 8][:, (k % 8) * 128:(k % 8 + 1) * 128], src))
            self.tr(specs, [XB.b], [bk[1].b, bk[2].b])

            if self._cut(3):
                return
            def xs_view(k0, k1):
                return pB[0][:, k0 * 128:k1 * 128]
            nx0 = min(NX, 8)
            xs3 = pB[0][:, 0:nx0 * 128].rearrange("p (h e) -> p h e", e=64)
            self.tt(Xb.ap.rearrange("p (h e) -> p h e", e=64), xs3,
                    dtv.ap.unsqueeze(2).to_broadcast([128, H, 64]), ALU.mult, [bk[1].b, dtv.b], [Xb.b])
            self.tt(Xw.ap.rearrange("p (h e) -> p h e", e=64), xs3,
                    dtw.ap.unsqueeze(2).to_broadcast([128, H, 64]), ALU.mult, [bk[1].b, dtw.b], [Xw.b])
            kb = NX
            bsrc = pB[kb // 8][:, (kb % 8) * 128:(kb % 8) * 128 + 256]
            self.cp(Btm.ap, bsrc, [bk[kb // 8 + 1].b], [Btm.b], eng="act")
            if self._cut(4):
                return
            self.mm([(b0.ap[:, 128 + g * 128:256 + g * 128], XB3[:, iB + g, :], XB3[:, iC + g, :], True, True)
                     for g in range(2)], [XB.b], [b0.b])
            for g in range(2):
                self.tt(cbm.ap[:, g * 128:(g + 1) * 128], b0.ap[:, 128 + g * 128:256 + g * 128], Mrhs, ALU.mult,
                        [b0.b, self.cst32.b], [cbm.b])
            if self._cut(5):
                return
            yoff = [bk[6], bk[7]]
            self.mm([(yoff[g].ap[:, :GW], XB3[:, iC + g, :], Sbf.ap[:, g * GW:(g + 1) * GW], True, True)
                     for g in range(2)], [XB.b, Sbf.b], [bk[6].b, bk[7].b])
            if self._cut(6):
                return
            ydg = [bk[4], bk[5]]
            for h0 in range(0, H, hb):
                g = h0 // HPG
                R3 = rhsD.ap.rearrange("p (h t) -> p h t", t=128)
                for hh in range(hb):
                    self.ts(R3[:, hh, :], Mrhs, da.ap[:, h0 + hh:h0 + hh + 1], ALU.mult, [da.b, self.cst32.b], [rhsD.b],
                            eng="pool" if hh % 2 else "dve")
                self.mm([(bk[3].ap[:, :hb * 128], Mseg, rhsD.ap, True, True)], [rhsD.b, self.cstbf.b], [bk[3].b])
                self.act(E.ap, bk[3].ap[:, :hb * 128], AF.Exp, [bk[3].b], [E.b])
                self.tt(MT.ap.rearrange("p (h t) -> p h t", t=128), E.ap.rearrange("p (h t) -> p h t", t=128),
                        cbm.ap[:, g * 128:(g + 1) * 128].unsqueeze(1).to_broadcast([128, hb, 128]), ALU.mult,
                        [E.b, cbm.b], [MT.b])
                M3 = MT.ap.rearrange("p (h t) -> p h t", t=128)
                specs = []
                for hh in range(hb):
                    h = h0 + hh
                    j = h - g * HPG
                    specs.append((ydg[g].ap[:, j * 64:(j + 1) * 64], M3[:, hh, :], Xb.ap[:, h * 64:(h + 1) * 64], True, True))
                self.mm(specs, [MT.b, Xb.b], [ydg[g].b])
            if self._cut(7):
                return
            for g in range(2):
                self.tt(tmpY.ap[:, g * GW:(g + 1) * GW].rearrange("p (h e) -> p h e", e=64),
                        yoff[g].ap[:, :GW].rearrange("p (h e) -> p h e", e=64),
                        ea.ap[:, g * HPG:(g + 1) * HPG].unsqueeze(2).to_broadcast([128, HPG, 64]), ALU.mult,
                        [yoff[g].b, ea.b], [tmpY.b])
                self.tt(Y.ap[:, g * GW:(g + 1) * GW], tmpY.ap[:, g * GW:(g + 1) * GW], ydg[g].ap[:, :GW], ALU.add,
                        [tmpY.b, ydg[g].b], [Y.b])
            if self._cut(8):
                return
            self.mm([(yoff[g].ap[:, :GW], Btm.ap[:, g * 128:(g + 1) * 128], Xw.ap[:, g * GW:(g + 1) * GW], True, True)
                     for g in range(2)], [Btm.b, Xw.b], [bk[6].b, bk[7].b])
            for g in range(2):
                sv = S32.ap[:, g * GW:(g + 1) * GW]
                self.tt(sv.rearrange("p (h e) -> p h e", e=64), sv.rearrange("p (h e) -> p h e", e=64),
                        dec.ap[:, g * HPG:(g + 1) * HPG].unsqueeze(2).to_broadcast([128, HPG, 64]), ALU.mult,
                        [S32.b, dec.b], [S32.b])
                self.tt(sv, sv, yoff[g].ap[:, :GW], ALU.add, [S32.b, yoff[g].b], [S32.b])
            self.cp(Sbf.ap, S32.ap, [S32.b], [Sbf.b], eng="act")
            if self._cut(9):
                return
            if d == 0:
                o_ds = c.b_dsk + i * H
                self.tt(tmpY.ap.rearrange("p (h e) -> p h e", e=64), xs3,
                        pbb.ap[:, o_ds:o_ds + H].unsqueeze(2).to_broadcast([128, H, 64]), ALU.mult,
                        [bk[1].b, pbb.b], [tmpY.b])
                self.tt(Y.ap, Y.ap, tmpY.ap, ALU.add, [Y.b, tmpY.b], [Y.b])
                self.dma(self.yf_tm[t0:t0 + 128, :], Y.ap, [Y.b], ())
            else:
                self.dma(YF.ap, self.yf_tm[t0:t0 + 128, :], (), [YF.b])
                self.dma(Zt.ap, self.z_tm[t0:t0 + 128, :], (), [Zt.b])
                self.tt(Y.ap, Y.ap, YF.ap, ALU.add, [Y.b, YF.b], [Y.b])
                self.act(sg.ap, Zt.ap, AF.Silu, [Zt.b], [sg.b])
                self.tt(Y.ap, Y.ap, sg.ap, ALU.mult, [Y.b, sg.b], [Y.b])
                self.memset(ssq.ap, 0.0, [ssq.b])
                for g in range(2):
                    self.act(junk.ap[:, :GW], Y.ap[:, g * GW:(g + 1) * GW], AF.Square, [Y.b], [junk.b, ssq.b],
                             accum=ssq.ap[:, g:g + 1])
                self.ts(rs.ap, ssq.ap, 1.0 / GW, ALU.mult, [ssq.b], [rs.b], s2=EPS, op1=ALU.add)
                self.act(rs.ap, rs.ap, AF.Sqrt, [rs.b], [rs.b])
                self.recip(rs.ap, rs.ap, [rs.b], [rs.b])
                o_nw = c.b_snw + i * INNER
                for g in range(2):
                    self.stt(Yn.ap[:, g * GW:(g + 1) * GW], Y.ap[:, g * GW:(g + 1) * GW], rs.ap[:, g:g + 1],
                             pbb.ap[:, o_nw + g * GW:o_nw + (g + 1) * GW], ALU.mult, ALU.mult, [Y.b, rs.b, pbb.b], [Yn.b])
                pT = bk[3].ap.bitcast(BF16)
                self.tr([(pT[:, k * 128:(k + 1) * 128], Yn.ap[:, k * 128:(k + 1) * 128]) for k in range(NX)],
                        [Yn.b], [bk[3].b])
                self.cp(YT.ap, pT[:, :INNER], [bk[3].b], [YT.b], eng="act")
                self.dma(self.mixT[0:INNER, t0:t0 + 128].rearrange("(k p) t -> p k t", p=128),
                         YT.ap.rearrange("p (k t) -> p k t", t=128), [YT.b], ())

    def gelu(self, out, x, n, tl_x, tl_out, t1, t2):
        self.act(t1.ap[:, :n], x, AF.Square, [tl_x.b], [t1.b])
        self.ts(t1.ap[:, :n], t1.ap[:, :n], 0.044715, ALU.mult, [t1.b], [t1.b], s2=1.0, op1=ALU.add)
        self.tt(t1.ap[:, :n], t1.ap[:, :n], x, ALU.mult, [t1.b, tl_x.b], [t1.b])
        self.act(t2.ap[:, :n], t1.ap[:, :n], AF.Sigmoid, [t1.b], [t2.b], scale=1.5957691216057308)
        self.tt(out, t2.ap[:, :n], x, ALU.mult, [t2.b, tl_x.b], [tl_out.b], eng="pool")

    def p_sgu(self, i):
        c = self.c
        SW, SG, INNER = c.SW, c.SG, c.INNER
        self.phase()
        ar = self.ar
        WT = ar.bf(SW)
        self.dma(WT.ap, self.i_sguw[i, :, :], (), [WT.b], q="pool")
        V = ar.bf(SW); U = ar.bf(SW)
        t1 = ar.f32(SW); t2 = ar.f32(SW); gv = ar.f32(SW); gu = ar.f32(SW)
        sm = ar.f32(1); mean = ar.f32(1); ssq = ar.f32(1); rs = ar.f32(1)
        cen = ar.f32(SW); vn = ar.bf(SW); mx = ar.f32(SW); yo = ar.bf(SW); junk = ar.f32(SW)
        o_b = c.b_sgub + i * SW
        nb = (SW * 4 + 2047) // 2048
        for ch in range(c.NT // 128):
            t0 = ch * 128
            self.dma(V.ap, self.v_tm[t0:t0 + 128, :], (), [V.b])
            self.dma(U.ap.rearrange("p (k t) -> p k t", t=128),
                     self.uT[:, t0:t0 + 128].rearrange("(k p) t -> p k t", p=128), (), [U.b])
            self.gelu(gv.ap, V.ap, SW, V, gv, t1, t2)
            self.gelu(gu.ap, U.ap, SW, U, gu, t1, t2)
            def fn_red(e, o=sm.ap, a=gv.ap):
                return e.tensor_reduce(out=o, in_=a, axis=AX.X, op=ALU.add)
            self.s.op("dve", fn_red, [gv.b], [sm.b])
            self.ts(mean.ap, sm.ap, 1.0 / SW, ALU.mult, [sm.b], [mean.b])
            self.ts(cen.ap, gv.ap, mean.ap[:, 0:1], ALU.subtract, [gv.b, mean.b], [cen.b])
            self.memset(ssq.ap, 0.0, [ssq.b])
            self.act(junk.ap, cen.ap, AF.Square, [cen.b], [junk.b, ssq.b], accum=ssq.ap[:, 0:1])
            self.ts(rs.ap, ssq.ap, 1.0 / SW, ALU.mult, [ssq.b], [rs.b], s2=EPS, op1=ALU.add)
            self.act(rs.ap, rs.ap, AF.Sqrt, [rs.b], [rs.b])
            self.recip(rs.ap, rs.ap, [rs.b], [rs.b])
            self.ts(vn.ap, cen.ap, rs.ap[:, 0:1], ALU.mult, [cen.b, rs.b], [vn.b])
            bks = [self.bank[j] for j in range(nb)]
            specs = []
            for g in range(SG):
                specs.append((bks[g // 4].ap[:, (g % 4) * 128:(g % 4 + 1) * 128], vn.ap[:, g * 128:(g + 1) * 128],
                              WT.ap[:, g * 128:(g + 1) * 128], True, True))
            self.mm(specs, [vn.b, WT.b], [b.b for b in bks])
            for j in range(nb):
                w = min(512, SW - j * 512)
                self.tt(mx.ap[:, j * 512:j * 512 + w], bks[j].ap[:, :w], self.pb.ap[:, o_b + j * 512:o_b + j * 512 + w],
                        ALU.add, [bks[j].b, self.pb.b], [mx.b])
            self.tt(yo.ap, mx.ap, gu.ap, ALU.mult, [mx.b, gu.b], [yo.b])
            self.dma(self.mixT[INNER:INNER + SW, t0:t0 + 128].rearrange("(k p) t -> p k t", p=128),
                     yo.ap.rearrange("p (k t) -> p k t", t=128), [yo.b], ())

    def odd_mixer(self, l):
        c = self.c
        i = l // 2
        W = self.i_oqkv[i, :, :]
        self.p_linear(self.hT, c.D, W, [
            (0, c.QC, "fm", self.qT, BF16),
            (c.QC, c.QC + c.KVC, "fm", self.kT, BF16),
            (c.QC + c.KVC, c.QKV, "tm", self.va_tm, BF16),
        ], c.TT)
        self.p_rope()
        self.p_attn(i)
        self.p_linear(self.mixT, c.D, self.i_owout[i, :, :], [(0, c.D, "fm", self.oT, F32)], c.TT)

    def p_rope(self):
        c = self.c
        self.phase()
        S, CTX = c.S, c.CTX
        COS = self.ar.f32(S)
        SIN = self.ar.f32(S)
        self.dma(COS.ap, self.i_rope[:, 0, :], (), [COS.b])
        self.dma(SIN.ap, self.i_rope[:, 1, :], (), [SIN.b])
        T = 512
        X = [self.ar.bf(T) for _ in range(2)]
        A = [self.ar.f32(T) for _ in range(2)]
        Bt = [self.ar.f32(T) for _ in range(2)]
        O = [self.ar.bf(T) for _ in range(2)]
        rp = self.mbf(6)
        it = 0
        for (src, nch) in ((self.qT, c.AH), (self.kT, c.KVH)):
            for hc in range(nch):
                for t0 in range(0, S, T):
                    n = min(T, S - t0)
                    x, a, b2, o = X[it % 2], A[it % 2], Bt[it % 2], O[it % 2]
                    bk = self.bank[it % 2]
                    it += 1
                    rows = src[hc * 128:(hc + 1) * 128, CTX + t0:CTX + t0 + n]
                    self.dma(x.ap[:, :n], rows, (), [x.b])
                    self.mm([(bk.ap[:, :n], rp, x.ap[:, :n], True, True)], [x.b, self.cstbf.b], [bk.b])
                    self.tt(a.ap[:, :n], x.ap[:, :n], COS.ap[:, t0:t0 + n], ALU.mult, [x.b, COS.b], [a.b], eng="pool")
                    self.tt(b2.ap[:, :n], bk.ap[:, :n], SIN.ap[:, t0:t0 + n], ALU.mult, [bk.b, SIN.b], [b2.b])
                    self.tt(o.ap[:, :n], a.ap[:, :n], b2.ap[:, :n], ALU.add, [a.b, b2.b], [o.b])
                    self.dma(rows, o.ap[:, :n], [o.b], ())

    def p_attn(self, i):
        c = self.c
        self.phase()
        NT, CTX, S = c.NT, c.CTX, c.S
        nblk = NT // 128
        nct = CTX // 128
        ar = self.ar
        KT = ar.bf(NT)
        Vt = ar.bf(NT)
        Q = [ar.bf(512) for _ in range(2)]
        PT = [ar.bf(5 * 512) for _ in range(2)]
        den = ar.f32(512); rc = ar.f32(512)
        PM = [ar.bf(512) for _ in range(2)]
        OUT = [ar.bf(512) for _ in range(2)]
        esk = ar.f32(c.AH)
        o_sk = c.b_sink + i * c.AH
        self.act(esk.ap, self.pb.ap[:, o_sk:o_sk + c.AH], AF.Exp, [self.pb.b], [esk.b])
        ones = self.mbf(5)
        scale = 1.0 / math.sqrt(128.0)
        it = 0
        for kh in range(c.KVH):
            self.dma(KT.ap, self.kT[kh * 128:(kh + 1) * 128, :], (), [KT.b])
            self.dma(Vt.ap.rearrange("p (b d) -> p b d", d=128),
                     self.va_tm[:, kh * 128:(kh + 1) * 128].rearrange("(b p) d -> p b d", p=128), (), [Vt.b])
            V3 = Vt.ap.rearrange("p (b d) -> p b d", d=128)
            import os
            for qb in range(min(nblk, int(os.environ.get("ATTQ", "999")))):
                q, pt, out = Q[it % 2], PT[it % 2], OUT[it % 2]
                it += 1
                t0 = qb * 128
                self.dma(q.ap.rearrange("p (g t) -> p g t", t=128),
                         self.qT[kh * 512:(kh + 1) * 512, t0:t0 + 128].rearrange("(g p) t -> p g t", p=128), (), [q.b])
                if self._acut(1):
                    return
                kbs = [(b, None) for b in range(nct)]
                if qb >= nct:
                    if qb - 1 >= nct:
                        kbs.append((qb - 1, 1))
                    kbs.append((qb, None))
                    if qb + 1 < nblk:
                        kbs.append((qb + 1, 0))
                P3 = pt.ap.rearrange("p (b n) -> p b n", n=512)
                for bi_, (kb, mk) in enumerate(kbs):
                    bkk = self.bank[bi_]
                    self.mm([(bkk.ap, KT.ap[:, kb * 128:(kb + 1) * 128], q.ap, True, True)], [KT.b, q.b], [bkk.b])
                    if mk is None:
                        self.act(P3[:, bi_, :], bkk.ap, AF.Exp, [bkk.b], [pt.b], scale=scale)
                    else:
                        pm = PM[bi_ % 2]
                        self.act(pm.ap, bkk.ap, AF.Exp, [bkk.b], [pm.b], scale=scale)
                        for g in range(4):
                            self.tt(P3[:, bi_, g * 128:(g + 1) * 128], pm.ap[:, g * 128:(g + 1) * 128], self.m32(mk),
                                    ALU.mult, [pm.b, self.cst32.b], [pt.b])
                if self._acut(2):
                    return
                nk = len(kbs)
                bo, bd = self.bank[5], self.bank[6]
                specs = [(bo.ap, V3[:, kb, :], P3[:, bi_, :], bi_ == 0, bi_ == nk - 1) for bi_, (kb, mk) in enumerate(kbs)]
                specs += [(bd.ap, ones, P3[:, bi_, :], bi_ == 0, bi_ == nk - 1) for bi_ in range(nk)]
                self.mm(specs, [Vt.b, pt.b, self.cstbf.b], [bo.b, bd.b])
                if self._acut(3):
                    return
                self.tt(den.ap.rearrange("p (g t) -> p g t", t=128), bd.ap.rearrange("p (g t) -> p g t", t=128),
                        esk.ap[:, kh * 4:kh * 4 + 4].unsqueeze(2).to_broadcast([128, 4, 128]), ALU.add,
                        [bd.b, esk.b], [den.b])
                if self._acut(4):
                    return
                self.recip(rc.ap, den.ap, [den.b], [rc.b])
                self.tt(out.ap, bo.ap, rc.ap, ALU.mult, [bo.b, rc.b], [out.b])
                if self._acut(5):
                    return
                self.dma(self.mixT[kh * 512:(kh + 1) * 512, t0:t0 + 128].rearrange("(g p) t -> p g t", p=128),
                         out.ap.rearrange("p (g t) -> p g t", t=128), [out.b], ())


def host_consts(cfg):
    p = np.arange(128)[:, None]
    f = np.arange(128)[None, :]
    cst = np.zeros((128, 7, 128), np.float32)
    cst[:, 0] = (p <= f)
    cst[:, 1] = (p >= f)
    cst[:, 2] = (p > f)
    cst[:, 3] = (p < f)
    cst[:, 4] = (p == f)
    cst[:, 5] = 1.0
    partner = np.where((np.arange(128) % 64) < 32, np.arange(128) + 32, np.arange(128) - 32)
    cst[:, 6] = (p == partner[None, :])
    S = cfg.S
    rows = S // cfg.GRID_W
    row = np.repeat(np.arange(rows, dtype=np.float32), cfg.GRID_W)
    col = np.tile(np.arange(cfg.GRID_W, dtype=np.float32), rows)
    inv = (np.float32(10000.0) ** (-np.arange(32, dtype=np.float32) / np.float32(32))).astype(np.float32)
    rope = np.zeros((128, 2, S), np.float32)
    for d in range(128):
        axis, half, fq = d // 64, (d % 64) // 32, d % 32
        pos = row if axis == 0 else col
        ang = (pos * inv[fq]).astype(np.float32)
        rope[d, 0] = np.cos(ang)
        rope[d, 1] = np.sin(ang) * (-1.0 if half == 0 else 1.0)
    return cst.reshape(128, 7 * 128), rope


def fm(v, kc):
    v = np.asarray(v, np.float32)
    lead = v.shape[:-1]
    return np.moveaxis(v.reshape(lead + (kc, 128)), -1, 0)


def host_params(cfg, b, inp):
    c = cfg
    KC = c.KC
    pp = np.zeros((128, c.NPP), np.float32)
    cc = np.stack([fm(inp["c"][b], KC), fm(inp["c_ctx"], KC)], axis=-1)
    pp[:, c.o_c:c.o_c + KC * 2] = cc.reshape(128, -1)
    pp[:, c.o_bmod:c.o_bmod + c.DEPTH * 6 * KC] = fm(inp["b_mod"], 6 * KC).reshape(128, -1)
    pp[:, c.o_nw:c.o_nw + c.DEPTH * 4 * KC] = fm(inp["norm_w"], KC).reshape(128, -1)
    cw = fm(inp["e_conv_w"], c.XBCC)
    pp[:, c.o_cw:c.o_cw + c.NE * c.XBCC * 5] = np.transpose(cw, (0, 1, 3, 2)).reshape(128, -1)
    pp[:, c.o_cb:c.o_cb + c.NE * c.XBCC] = fm(inp["e_conv_b"], c.XBCC).reshape(128, -1)
    pb = np.zeros((c.NPB,), np.float32)
    pb[c.b_dtb:c.b_dtb + c.NE * 2 * c.H] = np.asarray(inp["e_dt_bias"], np.float32).reshape(-1)
    pb[c.b_alog:c.b_alog + c.NE * 2 * c.H] = np.asarray(inp["e_a_log"], np.float32).reshape(-1)
    pb[c.b_dsk:c.b_dsk + c.NE * c.H] = np.asarray(inp["e_d_skip"], np.float32).reshape(-1)
    pb[c.b_snw:c.b_snw + c.NE * c.INNER] = np.asarray(inp["e_ssd_norm_w"], np.float32).reshape(-1)
    pb[c.b_sgub:c.b_sgub + c.NE * c.SW] = np.asarray(inp["e_sgu_b"], np.float32).reshape(-1)
    if c.NO:
        pb[c.b_sink:c.b_sink + c.NO * c.AH] = np.asarray(inp["o_sink"], np.float32).reshape(-1)
    pbb = np.ascontiguousarray(np.broadcast_to(pb[None, :], (128, c.NPB)))
    return pp, pbb


def run_cfg(cfg, inp, dbg=(), ncores=None, trace=False, stop_at=None):
    B = inp["x"].shape[0]
    bld = Builder(cfg, dbg, stop_at)
    nc = bld.build()
    cst, rope = host_consts(cfg)
    sguwT = np.ascontiguousarray(np.transpose(np.asarray(inp["e_sgu_w"], np.float32), (0, 3, 1, 2))
                                 .reshape(cfg.NE, 128, cfg.SW))
    shared = {
        "cst": cst, "rope": rope, "sguwT": sguwT,
        "w_mod": np.asarray(inp["w_mod"], np.float32), "w_ffn_in": np.asarray(inp["w_ffn_in"], np.float32),
        "w_ffn_out": np.asarray(inp["w_ffn_out"], np.float32), "e_w_in": np.asarray(inp["e_w_in"], np.float32),
        "e_w_out": np.asarray(inp["e_w_out"], np.float32), "o_w_qkv": np.asarray(inp["o_w_qkv"], np.float32),
        "o_w_out": np.asarray(inp["o_w_out"], np.float32),
    }
    if cfg.NO == 0:
        shared["o_w_qkv"] = np.zeros((1, cfg.D, cfg.QKV), np.float32)
        shared["o_w_out"] = np.zeros((1, cfg.D, cfg.D), np.float32)
    ncores = ncores or B
    in_maps = []
    for core in range(ncores):
        b = core % B
        pp, pbb = host_params(cfg, b, inp)
        xT = np.ascontiguousarray(np.concatenate([np.asarray(inp["ctx"][b], np.float32).T,
                                                  np.asarray(inp["x"][b], np.float32).T], axis=1))
        m = dict(shared)
        m.update({"xT": xT, "pp": pp, "pb": pbb})
        in_maps.append(m)
    res = run_bass_kernel_spmd(nc, in_maps, core_ids=list(range(ncores)), trace=trace)
    return res, bld


def kernel(**inputs):
    cfg = Cfg()
    res, _ = run_cfg(cfg, inputs, ncores=4)
    out = np.stack([np.ascontiguousarray(res.results[b]["y"].T) for b in range(4)], axis=0)
    return out.astype(np.float32)
```
